# Optimizing a Trainium2 kernel written in Bass

```python
import jax, jax.numpy as jnp
from jax import lax
import numpy as np

D_MODEL = 2048
BATCH = 1
SEQ = 8192
DEPTH = 2

N_MIXERS = 2
N_RET_LAYERS = (DEPTH + 1) // 2
N_ATT_LAYERS = DEPTH // 2

RET_HEADS = 8
RET_QK_DIM = D_MODEL // RET_HEADS
RET_V_DIM = 2 * D_MODEL // RET_HEADS
RET_CHUNK = 128
RET_IN_DIM = 2 * RET_HEADS * RET_QK_DIM + 2 * RET_HEADS * RET_V_DIM

ATT_HEAD_DIM = 64
ATT_HEADS = D_MODEL // ATT_HEAD_DIM
ATT_KV_HEADS = ATT_HEADS // 8
ATT_GROUP = ATT_HEADS // ATT_KV_HEADS
WINDOW = 128
ATT_BLOCK = 128
ATT_IN_DIM = (ATT_HEADS + 2 * ATT_KV_HEADS) * ATT_HEAD_DIM

D_FF = ((8 * D_MODEL // 3 + 127) // 128) * 128
CONV_WIDTH = 3

EPS = 1e-6

kernel_name = "hybrid_retention_swa_sink_convffn"


def rmsnorm(x, g):
    xf = x.astype(jnp.float32)
    y = xf * lax.rsqrt(jnp.mean(xf * xf, axis=-1, keepdims=True) + EPS)
    return (y * g.astype(jnp.float32)).astype(x.dtype)


def retention(h, w_in, w_out):
    B, S, _ = h.shape
    H, dk, dv, C = RET_HEADS, RET_QK_DIM, RET_V_DIM, RET_CHUNK
    nc = S // C
    proj = h @ w_in
    q, k, v, g = jnp.split(proj, [H * dk, 2 * H * dk, 2 * H * dk + H * dv], axis=-1)

    def to_chunks(t, d):
        return t.reshape(B, nc, C, H, d).transpose(1, 0, 3, 2, 4)

    q = to_chunks(q, dk)
    k = to_chunks(k, dk) * (dk ** -0.5)
    v = to_chunks(v, dv)

    log_gamma = jnp.log1p(-(2.0 ** (-5.0 - jnp.arange(H, dtype=jnp.float32))))
    n = jnp.arange(C, dtype=jnp.float32)
    diff = n[:, None] - n[None, :]
    intra = jnp.where(diff >= 0, jnp.exp(log_gamma[:, None, None] * jnp.maximum(diff, 0.0)), 0.0)
    q_decay = jnp.exp(log_gamma[:, None] * (n + 1.0))[..., None]
    k_decay = jnp.exp(log_gamma[:, None] * (C - 1.0 - n))[..., None]
    chunk_decay = jnp.exp(log_gamma * C)[:, None, None]

    def step(state, qkv):
        qc, kc, vc = qkv
        scores = jnp.einsum('bhnd,bhmd->bhnm', qc, kc) * intra
        o = (jnp.einsum('bhnm,bhme->bhne', scores, vc)
             + jnp.einsum('bhnd,bhde->bhne', qc * q_decay, state))
        state = state * chunk_decay + jnp.einsum('bhmd,bhme->bhde', kc * k_decay, vc)
        return state, o

    state0 = jnp.zeros((B, H, dk, dv), jnp.float32)
    _, o = lax.scan(step, state0, (q, k, v))
    o = o.transpose(1, 0, 3, 2, 4).reshape(B, S, H, dv).astype(jnp.float32)
    o = o * lax.rsqrt(jnp.mean(o * o, axis=-1, keepdims=True) + EPS)
    y = jax.nn.silu(g.astype(jnp.float32)) * o.reshape(B, S, H * dv)
    return y.astype(h.dtype) @ w_out


def sliding_window_attention(h, w_qkv, b_qkv, sinks, w_out):
    B, S, _ = h.shape
    Hq, Hkv, G, dh, BLK = ATT_HEADS, ATT_KV_HEADS, ATT_GROUP, ATT_HEAD_DIM, ATT_BLOCK
    nb = S // BLK
    proj = h @ w_qkv + b_qkv
    q, k, v = jnp.split(proj, [Hq * dh, (Hq + Hkv) * dh], axis=-1)
    q = q.reshape(B, nb, BLK, Hkv, G, dh)
    k = k.reshape(B, nb, BLK, Hkv, dh)
    v = v.reshape(B, nb, BLK, Hkv, dh)
    pad = ((0, 0), (1, 0), (0, 0), (0, 0), (0, 0))
    kb = jnp.concatenate([jnp.pad(k, pad)[:, :-1], k], axis=2)
    vb = jnp.concatenate([jnp.pad(v, pad)[:, :-1], v], axis=2)

    scores = jnp.einsum('bnqhgd,bnkhd->bnhgqk', q, kb).astype(jnp.float32) * (dh ** -0.5)
    qpos = jnp.arange(BLK) + BLK
    kpos = jnp.arange(2 * BLK)
    dist = qpos[:, None] - kpos[None, :]
    in_window = (dist >= 0) & (dist < WINDOW)
    blk = jnp.arange(nb)
    valid = in_window[None] & ((blk[:, None, None] > 0) | (kpos[None, None, :] >= BLK))
    slopes = (2.0 ** (-8.0 * jnp.arange(1, Hq + 1, dtype=jnp.float32) / Hq)).reshape(Hkv, G)
    scores = scores - slopes[:, :, None, None] * dist.astype(jnp.float32)
    scores = jnp.where(valid[None, :, None, None], scores, -jnp.inf)
    sink = jnp.broadcast_to(sinks.astype(jnp.float32).reshape(Hkv, G)[None, None, :, :, None, None],
                            scores.shape[:-1] + (1,))
    probs = jax.nn.softmax(jnp.concatenate([scores, sink], axis=-1), axis=-1)[..., :-1]
    out = jnp.einsum('bnhgqk,bnkhd->bnqhgd', probs.astype(vb.dtype), vb)
    return out.reshape(B, S, Hq * dh) @ w_out


def conv_ffn(h, w_up, conv_w, conv_b, w_down):
    S = h.shape[1]
    u = h @ w_up
    up = jnp.pad(u, ((0, 0), (CONV_WIDTH - 1, 0), (0, 0)))
    c = conv_b + sum(up[:, j:j + S] * conv_w[j] for j in range(CONV_WIDTH))
    a, b = jnp.split(c, 2, axis=-1)
    return (jax.nn.silu(a) * b) @ w_down


def setup_inputs(seed: int = 0) -> dict:
    key = jax.random.key(seed)
    ks = jax.random.split(key, 14)
    f32 = jnp.float32

    def dense(k, shape, fan_in):
        return jax.random.normal(k, shape, f32) * (fan_in ** -0.5)

    def gain(k, shape):
        return 1.0 + 0.02 * jax.random.normal(k, shape, f32)

    NR, NA = N_RET_LAYERS, N_ATT_LAYERS
    return {
        "x": jax.random.normal(ks[0], (BATCH, SEQ, D_MODEL), f32),
        "norm_mix_g": gain(ks[1], (DEPTH, D_MODEL)),
        "ret_w_in": dense(ks[2], (NR, D_MODEL, RET_IN_DIM), D_MODEL),
        "ret_w_out": dense(ks[3], (NR, RET_HEADS * RET_V_DIM, D_MODEL), RET_HEADS * RET_V_DIM),
        "att_w_qkv": dense(ks[4], (NA, D_MODEL, ATT_IN_DIM), D_MODEL),
        "att_b_qkv": 0.02 * jax.random.normal(ks[5], (NA, ATT_IN_DIM), f32),
        "att_sinks": jax.random.normal(ks[6], (NA, ATT_HEADS), f32),
        "att_w_out": dense(ks[7], (NA, ATT_HEADS * ATT_HEAD_DIM, D_MODEL), ATT_HEADS * ATT_HEAD_DIM),
        "norm_ffn_g": gain(ks[8], (DEPTH, D_MODEL)),
        "ffn_w_up": dense(ks[9], (DEPTH, D_MODEL, 2 * D_FF), D_MODEL),
        "ffn_conv_w": dense(ks[10], (DEPTH, CONV_WIDTH, 2 * D_FF), CONV_WIDTH),
        "ffn_conv_b": 0.02 * jax.random.normal(ks[11], (DEPTH, 2 * D_FF), f32),
        "ffn_w_down": dense(ks[12], (DEPTH, D_FF, D_MODEL), D_FF),
        "final_norm_g": gain(ks[13], (D_MODEL,)),
    }


def reference(x, norm_mix_g, ret_w_in, ret_w_out, att_w_qkv, att_b_qkv, att_sinks, att_w_out,
              norm_ffn_g, ffn_w_up, ffn_conv_w, ffn_conv_b, ffn_w_down, final_norm_g):
    h = x
    for i in range(DEPTH):
        j = i // N_MIXERS
        hn = rmsnorm(h, norm_mix_g[i])
        if i % N_MIXERS == 0:
            h = h + retention(hn, ret_w_in[j], ret_w_out[j])
        else:
            h = h + sliding_window_attention(hn, att_w_qkv[j], att_b_qkv[j], att_sinks[j], att_w_out[j])
        hn = rmsnorm(h, norm_ffn_g[i])
        h = h + conv_ffn(hn, ffn_w_up[i], ffn_conv_w[i], ffn_conv_b[i], ffn_w_down[i])
    return rmsnorm(h, final_norm_g)
```

```python
import numpy as np
import concourse.bass as bass
import concourse.mybir as mybir

F32 = mybir.dt.float32
BF16 = mybir.dt.bfloat16
ALU = mybir.AluOpType
AF = mybir.ActivationFunctionType
AX = mybir.AxisListType

PE, ACT, DVE, POOL, SP = "tensor", "scalar", "vector", "gpsimd", "sync"
ENGINES = (PE, ACT, DVE, POOL, SP)


class Buf:
    __slots__ = ("name", "last_writer", "readers")

    def __init__(self, name):
        self.name = name
        self.last_writer = None
        self.readers = {}


class Instr:
    __slots__ = ("eng", "fn", "deps", "is_dma", "sem_key", "sig", "needed", "idx")

    def __init__(self, eng, fn, is_dma, sem_key):
        self.eng = eng
        self.fn = fn
        self.deps = []
        self.is_dma = is_dma
        self.sem_key = sem_key
        self.sig = None
        self.needed = False


class Prog:
    def __init__(self, nc):
        self.nc = nc
        self.lists = {e: [] for e in ENGINES}
        self.n = 0
        self.final_dmas = []

    def add(self, eng, fn, reads=(), writes=(), dma=False, sem_key=None, final=False, excl=()):
        writes = tuple(writes) + tuple(excl)
        ins = Instr(eng, fn, dma, sem_key)
        ins.idx = self.n
        self.n += 1
        deps = {}
        for b in tuple(reads) + tuple(writes):
            w = b.last_writer
            if w is not None:
                deps[w.idx] = w
        for b in writes:
            for r in b.readers.values():
                deps[r.idx] = r
        for d in deps.values():
            if d is ins:
                continue
            if (not d.is_dma) and d.eng == eng and eng == PE and not dma:
                continue
            d.needed = True
            ins.deps.append(d)
        for b in writes:
            b.last_writer = ins
            b.readers = {}
        for b in reads:
            key = ("dma", ins.idx) if dma else eng
            b.readers[key] = ins
        if dma:
            ins.needed = True
            if sem_key is None:
                assert len(writes) >= 1
                ins.sem_key = writes[0].name
        self.lists[eng].append(ins)
        if final:
            self.final_dmas.append(ins)
        return ins

    def emit(self):
        nc = self.nc
        counters = {}
        sem_names = {}
        for e in ENGINES:
            for ins in self.lists[e]:
                if not ins.needed:
                    continue
                if ins.is_dma:
                    key = ("dma", ins.sem_key)
                    counters[key] = counters.get(key, 0) + 16
                else:
                    key = ("eng", e)
                    counters[key] = counters.get(key, 0) + 1
                ins.sig = (key, counters[key])
                sem_names[key] = None
        keys = list(sem_names.keys())
        self.n_sems = len(keys)
        import contextlib
        with contextlib.ExitStack() as st:
            sems = {}
            for i, k in enumerate(keys):
                sems[k] = st.enter_context(nc.semaphore("s%d" % i))
            block = st.enter_context(nc.Block())
            finals = self.final_dmas

            def make(e):
                def body(eng):
                    waited = {}
                    for ins in self.lists[e]:
                        for d in ins.deps:
                            k, v = d.sig
                            if waited.get(k, 0) >= v:
                                continue
                            waited[k] = v
                            eng.wait_ge(sems[k], v)
                        r = ins.fn(eng)
                        if ins.sig is not None:
                            r.then_inc(sems[ins.sig[0]], 16 if ins.is_dma else 1)
                    if e == SP:
                        for d in finals:
                            k, v = d.sig
                            if waited.get(k, 0) >= v:
                                continue
                            waited[k] = v
                            eng.wait_ge(sems[k], v)
                return body

            for e in ENGINES:
                getattr(block, e)(make(e))


import contextlib
import numpy as np

EPS = 1e-6
S = 8192
D = 2048
KT = 16


class Ring:
    def __init__(self, items):
        self.items = items
        self.i = 0

    def next(self):
        it = self.items[self.i]
        self.i = (self.i + 1) % len(self.items)
        return it


def build_L1(n_groups=S // 512, stage=99):
    nc = bass.Bass("TRN2", target_bir_lowering=False)
    Sx = n_groups * 512
    xT = nc.dram_tensor("xT", [D, Sx], F32, kind="ExternalInput").ap()
    w = nc.dram_tensor("w", [128, KT, 1536], F32, kind="ExternalInput").ap()
    gn = nc.dram_tensor("gn", [128, KT], F32, kind="ExternalInput").ap()
    NCST = 128 + 512 + 2
    cst = nc.dram_tensor("cst", [128, NCST], F32, kind="ExternalInput").ap()
    y = nc.dram_tensor("y", [Sx, 512], F32, kind="ExternalOutput").ap()
    xTv = xT.rearrange("(kt p) s -> p kt s", p=128)
    with contextlib.ExitStack() as st:
        sb = lambda name, shape, dt: st.enter_context(nc.sbuf_tensor(name, shape, dt))
        wsb = sb("wsb", [128, KT, 1536], BF16)
        gsb = sb("gsb", [128, KT], F32)
        csb = sb("csb", [128, NCST], F32)
        ones = sb("ones", [128, 128], BF16)
        xg = sb("xg", [128, KT, 512], F32)
        sqs = [sb("sq%d" % i, [128, 512], BF16) for i in range(4)]
        rstd = sb("rstd", [128, 512], F32)
        hns = [sb("hn%d" % i, [128, KT, 512], BF16) for i in range(2)]
        qTs = [sb("qT%d" % i, [128, 2, 512], BF16) for i in range(2)]
        qdTs = [sb("qdT%d" % i, [128, 2, 512], BF16) for i in range(2)]
        kTs = [sb("kT%d" % i, [128, 2, 512], BF16) for i in range(2)]
        vcs = [sb("vc%d" % i, [128, 512], BF16) for i in range(2)]
        sgs = [sb("sg%d" % i, [128, 512], F32) for i in range(2)]
        kds = [sb("kd%d" % i, [128, 256], BF16) for i in range(2)]
        scs = [sb("sc%d" % i, [128, 128], BF16) for i in range(2)]
        S32 = sb("S32", [128, 2, 512], F32)
        Sbf = sb("Sbf", [128, 2, 512], BF16)
        junk = sb("junk", [128, 512], F32)
        ssq = [sb("ssq%d" % i, [128, 1], F32) for i in range(2)]
        rs = [sb("rs%d" % i, [128, 1], F32) for i in range(2)]
        ycs = [sb("yc%d" % i, [128, 512], F32) for i in range(2)]
        pst = [st.enter_context(nc.psum_tensor("ps%d" % i, [128, 512], F32)) for i in range(8)]

        P = Prog(nc)
        psr = Ring([(pst[i], Buf("ps%d" % i)) for i in range(8)])
        b_w, b_g, b_c, b_ones = Buf("w"), Buf("g"), Buf("c"), Buf("ones")
        b_xg = [Buf("xg%d" % k) for k in range(KT)]
        sqr = Ring([(sqs[i], Buf("sq%d" % i)) for i in range(4)])
        b_rstd = Buf("rstd")
        hnr = Ring([(hns[i], [Buf("hn%d_%d" % (i, k)) for k in range(KT)]) for i in range(2)])
        qr = Ring([(qTs[i], qdTs[i], kTs[i], [Buf("q%d_%d" % (i, d)) for d in range(2)],
                    [Buf("qd%d_%d" % (i, d)) for d in range(2)], [Buf("k%d_%d" % (i, d)) for d in range(2)])
                   for i in range(2)])
        cr = Ring([(vcs[i], sgs[i], kds[i], scs[i], ssq[i], rs[i], ycs[i],
                    Buf("vc%d" % i), Buf("sg%d" % i), Buf("kd%d" % i), Buf("sc%d" % i),
                    Buf("ssq%d" % i), Buf("rs%d" % i), Buf("yc%d" % i)) for i in range(2)])
        b_S32 = [Buf("S32_%d" % d) for d in range(2)]
        b_Sbf = [Buf("Sbf_%d" % d) for d in range(2)]
        b_junk = Buf("junk")

        maskT = csb[:, 0:128]
        qdec4 = csb[:, 128:640]
        kdec = csb[:, 640:641]
        cdec = csb[:, 641:642]

        for kq in range(KT):
            P.add(POOL, lambda e, kq=kq: e.dma_start(out=wsb[:, kq, :], in_=w[:, kq, :]),
                  writes=[b_w], dma=True, sem_key="w")
        P.add(SP, lambda e: e.dma_start(out=gsb[:, :], in_=gn[:, :]), writes=[b_g], dma=True, sem_key="g")
        P.add(SP, lambda e: e.dma_start(out=csb[:, :], in_=cst[:, :]), writes=[b_c], dma=True, sem_key="c")
        P.add(DVE, lambda e: e.memset(ones[:, :], 1.0), writes=[b_ones])
        for d in range(2):
            P.add(DVE, lambda e, d=d: e.memset(S32[:, d, :], 0.0), writes=[b_S32[d]])
            P.add(DVE, lambda e, d=d: e.memset(Sbf[:, d, :], 0.0), writes=[b_Sbf[d]])

        for G in range(n_groups):
            t0 = G * 512
            for kt in range(KT):
                P.add(SP, lambda e, t0=t0, kt=kt: e.dma_start(out=xg[:, kt, :], in_=xTv[:, kt, t0:t0 + 512]),
                      writes=[b_xg[kt]], dma=True, sem_key=("xg", kt))
            pss, bpss = psr.next()
            for kt in range(KT):
                sq, bsq = sqr.next()
                P.add(ACT, lambda e, sq=sq, kt=kt: e.activation(sq[:, :], xg[:, kt, :], AF.Square),
                      reads=[b_xg[kt]], writes=[bsq])
                P.add(PE, lambda e, sq=sq, kt=kt, pss=pss: e.matmul(pss[:, :], ones[:, :], sq[:, :], start=(kt == 0), stop=(kt == KT - 1)),
                      reads=[bsq, b_ones], writes=[bpss])
            P.add(ACT, lambda e, pss=pss: e.activation(rstd[:, :], pss[:, :], AF.Sqrt, bias=EPS, scale=1.0 / D),
                  reads=[bpss], writes=[b_rstd])
            P.add(DVE, lambda e: e.reciprocal(rstd[:, :], rstd[:, :]),
                  reads=[b_rstd], writes=[b_rstd])
            hn, bhn = hnr.next()
            for kt in range(KT):
                P.add(DVE, lambda e, hn=hn, kt=kt: e.scalar_tensor_tensor(
                    out=hn[:, kt, :], in0=xg[:, kt, :], scalar=gsb[:, kt:kt + 1], in1=rstd[:, :],
                    op0=ALU.mult, op1=ALU.mult),
                    reads=[b_xg[kt], b_rstd, b_g], writes=[bhn[kt]])
            if stage <= 1:
                continue
            qT, qdT, kT, bq, bqd, bk = qr.next()
            for which in ("q", "k"):
                col0 = 0 if which == "q" else 256
                for dt in range(2):
                    ps, bps = psr.next()
                    for kt in range(KT):
                        P.add(PE, lambda e, ps=ps, kt=kt, c0=col0 + dt * 128, hn=hn: e.matmul(
                            ps[:, :], wsb[:, kt, c0:c0 + 128], hn[:, kt, :], start=(kt == 0), stop=(kt == KT - 1)),
                            reads=[b_w, bhn[kt]], writes=[bps])
                    if which == "q":
                        P.add(ACT, lambda e, ps=ps, dt=dt, qT=qT: e.activation(qT[:, dt, :], ps[:, :], AF.Copy),
                              excl=[bps], writes=[bq[dt]])
                        P.add(DVE, lambda e, ps=ps, dt=dt, qdT=qdT: e.tensor_tensor(qdT[:, dt, :], ps[:, :], qdec4, ALU.mult),
                              excl=[bps], reads=[b_c], writes=[bqd[dt]])
                    else:
                        P.add(ACT, lambda e, ps=ps, dt=dt, kT=kT: e.activation(kT[:, dt, :], ps[:, :], AF.Copy, scale=0.0625),
                              reads=[bps], writes=[bk[dt]])
            if stage <= 2:
                continue
            for c in range(4):
                cs = slice(c * 128, (c + 1) * 128)
                (vc, sg, kd, sc, ssq_c, rs_c, yc, b_vc, b_sg, b_kd, b_sc, b_ssq, b_rs, b_yc) = cr.next()
                psv, bpsv = psr.next()
                psg, bpsg = psr.next()
                psk, bpsk = psr.next()
                for kt in range(KT):
                    st_, sp_ = (kt == 0), (kt == KT - 1)
                    P.add(PE, lambda e, kt=kt, hn=hn, cs=cs, psv=psv, st_=st_, sp_=sp_: e.matmul(
                        psv[:, :], hn[:, kt, cs], wsb[:, kt, 512:1024], start=st_, stop=sp_),
                        reads=[b_w, bhn[kt]], writes=[bpsv])
                    P.add(PE, lambda e, kt=kt, hn=hn, cs=cs, psg=psg, st_=st_, sp_=sp_: e.matmul(
                        psg[:, :], hn[:, kt, cs], wsb[:, kt, 1024:1536], start=st_, stop=sp_),
                        reads=[b_w, bhn[kt]], writes=[bpsg])
                    P.add(PE, lambda e, kt=kt, hn=hn, cs=cs, psk=psk, st_=st_, sp_=sp_: e.matmul(
                        psk[:, 0:256], hn[:, kt, cs], wsb[:, kt, 256:512], start=st_, stop=sp_),
                        reads=[b_w, bhn[kt]], writes=[bpsk])
                P.add(ACT, lambda e, vc=vc, psv=psv: e.activation(vc[:, :], psv[:, :], AF.Copy), reads=[bpsv], writes=[b_vc])
                P.add(ACT, lambda e, sg=sg, psg=psg: e.activation(sg[:, :], psg[:, :], AF.Silu), reads=[bpsg], writes=[b_sg])
                P.add(DVE, lambda e, kd=kd, psk=psk: e.tensor_scalar(kd[:, :], psk[:, 0:256], kdec, None, ALU.mult),
                      reads=[bpsk, b_c], writes=[b_kd])
                if stage <= 3:
                    continue
                pssc, bpssc = psr.next()
                for dt in range(2):
                    P.add(PE, lambda e, dt=dt, pssc=pssc, kT=kT, qT=qT, cs=cs: e.matmul(
                        pssc[:, 0:128], kT[:, dt, cs], qT[:, dt, cs], start=(dt == 0), stop=(dt == 1)),
                        reads=[bk[dt], bq[dt]], writes=[bpssc])
                P.add(DVE, lambda e, sc=sc, pssc=pssc: e.tensor_tensor(sc[:, :], pssc[:, 0:128], maskT, ALU.mult),
                      reads=[bpssc, b_c], writes=[b_sc])
                pso, bpso = psr.next()
                P.add(PE, lambda e, pso=pso, sc=sc, vc=vc: e.matmul(pso[:, :], sc[:, :], vc[:, :], start=True, stop=False),
                      reads=[b_sc, b_vc], writes=[bpso])
                for dt in range(2):
                    P.add(PE, lambda e, dt=dt, pso=pso, qdT=qdT, cs=cs: e.matmul(
                        pso[:, :], qdT[:, dt, cs], Sbf[:, dt, :], start=False, stop=(dt == 1)),
                        reads=[bqd[dt], b_Sbf[dt]], writes=[bpso])
                for dt in range(2):
                    psd, bpsd = psr.next()
                    P.add(PE, lambda e, dt=dt, psd=psd, kd=kd, vc=vc: e.matmul(
                        psd[:, :], kd[:, dt * 128:(dt + 1) * 128], vc[:, :], start=True, stop=True),
                        reads=[b_kd, b_vc], writes=[bpsd])
                    P.add(DVE, lambda e, dt=dt, psd=psd: e.scalar_tensor_tensor(
                        out=S32[:, dt, :], in0=S32[:, dt, :], scalar=cdec, in1=psd[:, :], op0=ALU.mult, op1=ALU.add),
                        reads=[bpsd, b_c, b_Sbf[dt]], writes=[b_S32[dt]])
                    P.add(ACT, lambda e, dt=dt: e.activation(Sbf[:, dt, :], S32[:, dt, :], AF.Copy),
                          reads=[b_S32[dt]], writes=[b_Sbf[dt]])
                if stage <= 4:
                    continue
                P.add(ACT, lambda e, pso=pso, ssq_c=ssq_c: e.activation(junk[:, :], pso[:, :], AF.Square, accum_out=ssq_c[:, :]),
                      excl=[bpso], writes=[b_junk, b_ssq])
                P.add(ACT, lambda e, ssq_c=ssq_c, rs_c=rs_c: e.activation(rs_c[:, :], ssq_c[:, :], AF.Sqrt, bias=EPS, scale=1.0 / 512),
                      reads=[b_ssq], writes=[b_rs])
                P.add(DVE, lambda e, rs_c=rs_c: e.reciprocal(rs_c[:, :], rs_c[:, :]),
                      reads=[b_rs], writes=[b_rs])
                P.add(DVE, lambda e, yc=yc, pso=pso, rs_c=rs_c, sg=sg: e.scalar_tensor_tensor(
                    out=yc[:, :], in0=pso[:, :], scalar=rs_c[:, :], in1=sg[:, :], op0=ALU.mult, op1=ALU.mult),
                    excl=[bpso], reads=[b_rs, b_sg], writes=[b_yc])
                r0 = t0 + c * 128
                P.add(SP, lambda e, yc=yc, r0=r0: e.dma_start(out=y[r0:r0 + 128, :], in_=yc[:, :]),
                      reads=[b_yc], dma=True, sem_key=("y", id(yc)), final=True)
        P.emit()
    return nc


def l1_inputs(x, norm_g, w_in, h, n_groups=S // 512):
    Sx = n_groups * 512
    xT = np.ascontiguousarray(x[:Sx].T)
    cols = np.concatenate([w_in[:, h * 256:(h + 1) * 256], w_in[:, 2048 + h * 256:2048 + (h + 1) * 256],
                           w_in[:, 4096 + h * 512:4096 + (h + 1) * 512], w_in[:, 8192 + h * 512:8192 + (h + 1) * 512]], axis=1)
    wb = np.ascontiguousarray(cols.reshape(KT, 128, 1536).transpose(1, 0, 2))
    gn = np.ascontiguousarray(norm_g.reshape(KT, 128).T)
    lg = np.log1p(-(2.0 ** (-5.0 - h)))
    n = np.arange(128, dtype=np.float64)
    diff = n[None, :] - n[:, None]
    maskT = np.where(diff >= 0, np.exp(lg * np.maximum(diff, 0)), 0.0)
    qdec = np.exp(lg * (n + 1.0))
    kdec = np.exp(lg * (127.0 - n)) * 0.0625
    cd = np.exp(lg * 128.0)
    cst = np.zeros((128, 642), np.float32)
    cst[:, 0:128] = maskT
    cst[:, 128:640] = np.tile(qdec, 4)[None, :]
    cst[:, 640] = kdec
    cst[:, 641] = cd
    return {"xT": xT, "w": wb, "gn": gn, "cst": cst}


import contextlib
import numpy as np

T = 1024
DFF = 5504
NP = 43
PH = [(0, 11), (11, 22), (22, 33), (33, 43)]


def rmsnorm_T(P, nc, psr, sqr, hT, b_h, gsb, b_g, ones, b_ones, rstd, b_rstd, out, b_out, chunks, gcol0=0):
    for (c0, c1) in chunks:
        ps, bps = psr.next()
        n = c1 - c0
        for kt in range(KT):
            sq, bsq = sqr.next()
            P.add(ACT, lambda e, sq=sq, kt=kt, c0=c0, c1=c1, n=n: e.activation(sq[:, 0:n], hT[:, kt, c0:c1], AF.Square),
                  reads=[b_h[kt]], writes=[bsq])
            P.add(PE, lambda e, sq=sq, kt=kt, ps=ps, n=n: e.matmul(ps[:, 0:n], ones[:, :], sq[:, 0:n], start=(kt == 0), stop=(kt == KT - 1)),
                  reads=[bsq, b_ones], writes=[bps])
        P.add(ACT, lambda e, ps=ps, c0=c0, c1=c1, n=n: e.activation(rstd[:, c0:c1], ps[:, 0:n], AF.Sqrt, bias=EPS, scale=1.0 / D),
              excl=[bps], writes=[b_rstd])
    P.add(DVE, lambda e: e.reciprocal(rstd[:, :], rstd[:, :]), writes=[b_rstd])
    for kt in range(KT):
        P.add(DVE, lambda e, kt=kt: e.scalar_tensor_tensor(
            out=out[:, kt, :], in0=hT[:, kt, :], scalar=gsb[:, gcol0 + kt:gcol0 + kt + 1], in1=rstd[:, :],
            op0=ALU.mult, op1=ALU.mult), reads=[b_h[kt], b_rstd, b_g], writes=[b_out[kt]])


def build_LF(final_norm=False):
    nc = bass.Bass("TRN2", target_bir_lowering=False)
    NT = T + 2
    hin = nc.dram_tensor("hin", [D, NT], F32, kind="ExternalInput").ap()
    gn = nc.dram_tensor("gn", [128, 2 * KT], F32, kind="ExternalInput").ap()
    wup = nc.dram_tensor("wup", [NP, 128, KT * 256], F32, kind="ExternalInput").ap()
    cw = nc.dram_tensor("cw", [128, NP * 8], F32, kind="ExternalInput").ap()
    wdn = nc.dram_tensor("wdn", [32, 128, 11 * 256], F32, kind="ExternalInput").ap()
    hout = nc.dram_tensor("hout", [D, T], F32, kind="ExternalOutput").ap()
    hinv = hin.rearrange("(kt p) s -> p kt s", p=128)
    houtv = hout.rearrange("(kt p) s -> p kt s", p=128)
    with contextlib.ExitStack() as st:
        sb = lambda name, shape, dt: st.enter_context(nc.sbuf_tensor(name, shape, dt))
        hT = sb("hT", [128, KT, NT], F32)
        hnT = sb("hnT", [128, KT, NT], BF16)
        gsb = sb("gsb", [128, 2 * KT], F32)
        cwsb = sb("cwsb", [128, NP * 8], F32)
        ones = sb("ones", [128, 128], BF16)
        sqs = [sb("sq%d" % i, [128, 512], BF16) for i in range(4)]
        rstd = sb("rstd", [128, NT], F32)
        wups = [sb("wup%d" % i, [128, KT * 256], BF16) for i in range(2)]
        wdns = [sb("wdn%d" % i, [128, 11 * 256], BF16) for i in range(2)]
        uA = sb("uA", [128, NT], F32)
        uB = sb("uB", [128, NT], F32)
        cA = sb("cA", [128, T], F32)
        cB = sb("cB", [128, T], F32)
        zT = sb("zT", [128, 11, T], BF16)
        pst = [st.enter_context(nc.psum_tensor("ps%d" % i, [128, 512], F32)) for i in range(8)]
        P = Prog(nc)
        psr = Ring([(pst[i], Buf("ps%d" % i)) for i in range(8)])
        sqr = Ring([(sqs[i], Buf("sq%d" % i)) for i in range(4)])
        wupr = Ring([(wups[i], Buf("wup%d" % i)) for i in range(2)])
        wdnr = Ring([(wdns[i], Buf("wdn%d" % i)) for i in range(2)])
        b_h = [Buf("h%d" % k) for k in range(KT)]
        b_hn = [Buf("hn%d" % k) for k in range(KT)]
        b_g, b_cw, b_ones, b_rstd = Buf("g"), Buf("cw"), Buf("ones"), Buf("rstd")
        b_uA, b_uB = [Buf("uA%d" % i) for i in range(3)], [Buf("uB%d" % i) for i in range(3)]
        b_cA, b_cB = Buf("cA"), Buf("cB")
        b_z = [Buf("z%d" % i) for i in range(11)]

        P.add(SP, lambda e: e.dma_start(out=gsb[:, :], in_=gn[:, :]), writes=[b_g], dma=True, sem_key="g")
        P.add(SP, lambda e: e.dma_start(out=cwsb[:, :], in_=cw[:, :]), writes=[b_cw], dma=True, sem_key="cw")
        P.add(DVE, lambda e: e.memset(ones[:, :], 1.0), writes=[b_ones])
        for kt in range(KT):
            P.add(SP, lambda e, kt=kt: e.dma_start(out=hT[:, kt, :], in_=hinv[:, kt, :]), writes=[b_h[kt]], dma=True, sem_key=("h", kt))
        chunks = [(0, 512), (512, 1024), (1024, NT)]
        rmsnorm_T(P, nc, psr, sqr, hT, b_h, gsb, b_g, ones, b_ones, rstd, b_rstd, hnT, b_hn, chunks)

        def load_wup(i):
            wt, bw = wupr.next()
            P.add(POOL, lambda e, wt=wt, i=i: e.dma_start(out=wt[:, :], in_=wup[i, :, :]), writes=[bw], dma=True)
            return wt, bw

        def load_wdn(j):
            wt, bw = wdnr.next()
            P.add(POOL, lambda e, wt=wt, j=j: e.dma_start(out=wt[:, :], in_=wdn[j, :, :]), writes=[bw], dma=True)
            return wt, bw

        nxt_up = load_wup(0)
        for ph, (p0, p1) in enumerate(PH):
            for i in range(p0, p1):
                wt, bw = nxt_up
                if i + 1 < NP:
                    nxt_up = load_wup(i + 1)
                zi = i - p0
                for side, (u, b_u, c, b_c) in enumerate(((uA, b_uA, cA, b_cA), (uB, b_uB, cB, b_cB))):
                    banks = [psr.next() for _ in range(3)]
                    for kt in range(KT):
                        for (c0, c1), (ps, bps) in zip(chunks, banks):
                            P.add(PE, lambda e, kt=kt, ps=ps, wt=wt, side=side, c0=c0, c1=c1: e.matmul(
                                ps[:, 0:c1 - c0], wt[:, kt * 256 + side * 128: kt * 256 + side * 128 + 128], hnT[:, kt, c0:c1],
                                start=(kt == 0), stop=(kt == KT - 1)), reads=[bw, b_hn[kt]], writes=[bps])
                    for ci, ((c0, c1), (ps, bps)) in enumerate(zip(chunks, banks)):
                        P.add(ACT, lambda e, u=u, ps=ps, c0=c0, c1=c1: e.activation(u[:, c0:c1], ps[:, 0:c1 - c0], AF.Copy),
                              excl=[bps], writes=[b_u[ci]])
                    k0 = i * 8 + side * 4
                    P.add(DVE, lambda e, c=c, u=u, k0=k0: e.tensor_scalar(
                        c[:, :], u[:, 2:NT], cwsb[:, k0 + 2:k0 + 3], cwsb[:, k0 + 3:k0 + 4], ALU.mult, ALU.add),
                        reads=b_u + [b_cw], writes=[b_c])
                    P.add(DVE, lambda e, c=c, u=u, k0=k0: e.scalar_tensor_tensor(
                        out=c[:, :], in0=u[:, 1:NT - 1], scalar=cwsb[:, k0 + 1:k0 + 2], in1=c[:, :], op0=ALU.mult, op1=ALU.add),
                        reads=b_u + [b_cw], writes=[b_c])
                    P.add(DVE, lambda e, c=c, u=u, k0=k0: e.scalar_tensor_tensor(
                        out=c[:, :], in0=u[:, 0:NT - 2], scalar=cwsb[:, k0:k0 + 1], in1=c[:, :], op0=ALU.mult, op1=ALU.add),
                        reads=b_u + [b_cw], writes=[b_c])
                P.add(ACT, lambda e: e.activation(cA[:, :], cA[:, :], AF.Silu), writes=[b_cA])
                P.add(DVE, lambda e, zi=zi: e.tensor_tensor(zT[:, zi, :], cA[:, :], cB[:, :], ALU.mult),
                      reads=[b_cA, b_cB], writes=[b_z[zi]])
            nf = p1 - p0
            nxt_dn = load_wdn(ph * 8)
            for dm2 in range(8):
                wt, bw = nxt_dn
                if dm2 + 1 < 8:
                    nxt_dn = load_wdn(ph * 8 + dm2 + 1)
                for sub in range(2):
                    dmt = dm2 * 2 + sub
                    banks = [psr.next() for _ in range(2)]
                    for fi in range(nf):
                        for tb, (ps, bps) in enumerate(banks):
                            P.add(PE, lambda e, fi=fi, ps=ps, wt=wt, sub=sub, tb=tb, nf=nf: e.matmul(
                                ps[:, :], wt[:, fi * 256 + sub * 128: fi * 256 + sub * 128 + 128], zT[:, fi, tb * 512:(tb + 1) * 512],
                                start=(fi == 0), stop=(fi == nf - 1)), reads=[bw, b_z[fi]], writes=[bps])
                    for tb, (ps, bps) in enumerate(banks):
                        P.add(DVE, lambda e, dmt=dmt, ps=ps, tb=tb: e.tensor_tensor(
                            hT[:, dmt, 2 + tb * 512: 2 + (tb + 1) * 512], hT[:, dmt, 2 + tb * 512: 2 + (tb + 1) * 512], ps[:, :], ALU.add),
                            excl=[bps], writes=[b_h[dmt]])
        if final_norm:
            rmsnorm_T_final(P, nc, psr, sqr, hT, b_h, gsb, b_g, ones, b_ones, rstd, b_rstd, houtv, chunks, uA, b_uA[0], uB, b_uB[0])
        else:
            for kt in range(KT):
                P.add(SP, lambda e, kt=kt: e.dma_start(out=houtv[:, kt, :], in_=hT[:, kt, 2:NT]), reads=[b_h[kt]], dma=True,
                      sem_key=("o", kt), final=True)
        P.emit()
    return nc


def rmsnorm_T_final(P, nc, psr, sqr, hT, b_h, gsb, b_g, ones, b_ones, rstd, b_rstd, houtv, chunks, s0, b_s0, s1, b_s1):
    NT = T + 2
    for (c0, c1) in chunks:
        ps, bps = psr.next()
        n = c1 - c0
        for kt in range(KT):
            sq, bsq = sqr.next()
            P.add(ACT, lambda e, sq=sq, kt=kt, c0=c0, c1=c1, n=n: e.activation(sq[:, 0:n], hT[:, kt, c0:c1], AF.Square),
                  reads=[b_h[kt]], writes=[bsq])
            P.add(PE, lambda e, sq=sq, kt=kt, ps=ps, n=n: e.matmul(ps[:, 0:n], ones[:, :], sq[:, 0:n], start=(kt == 0), stop=(kt == KT - 1)),
                  reads=[bsq, b_ones], writes=[bps])
        P.add(ACT, lambda e, ps=ps, c0=c0, c1=c1, n=n: e.activation(rstd[:, c0:c1], ps[:, 0:n], AF.Sqrt, bias=EPS, scale=1.0 / D),
              excl=[bps], writes=[b_rstd])
    P.add(DVE, lambda e: e.reciprocal(rstd[:, :], rstd[:, :]), writes=[b_rstd])
    stg = Ring([(s0, b_s0), (s1, b_s1)])
    for kt in range(KT):
        s, b_s = stg.next()
        P.add(DVE, lambda e, kt=kt, s=s: e.scalar_tensor_tensor(
            out=s[:, :], in0=hT[:, kt, :], scalar=gsb[:, KT + kt:KT + kt + 1], in1=rstd[:, :],
            op0=ALU.mult, op1=ALU.mult), reads=[b_h[kt], b_rstd, b_g], writes=[b_s])
        P.add(SP, lambda e, kt=kt, s=s: e.dma_start(out=houtv[:, kt, :], in_=s[:, 2:NT]), reads=[b_s], dma=True,
              sem_key=("o", kt % 2), final=True)


def lf_weights(norm_g, final_g, w_up, conv_w, conv_b, w_down):
    gn = np.concatenate([norm_g.reshape(KT, 128).T, final_g.reshape(KT, 128).T], axis=1)
    a = w_up[:, :DFF].reshape(KT, 128, NP, 128)
    b = w_up[:, DFF:].reshape(KT, 128, NP, 128)
    wupb = np.concatenate([a, b], axis=3).transpose(2, 1, 0, 3).reshape(NP, 128, KT * 256)
    cwv = np.zeros((128, NP, 2, 4), np.float32)
    for side in range(2):
        off = side * DFF
        for j in range(3):
            cwv[:, :, side, j] = conv_w[j, off:off + DFF].reshape(NP, 128).T
        cwv[:, :, side, 3] = conv_b[off:off + DFF].reshape(NP, 128).T
    wd = np.zeros((4, 8, 128, 11, 256), np.float32)
    wdr = w_down.reshape(NP, 128, 8, 256)
    for ph, (p0, p1) in enumerate(PH):
        wd[ph, :, :, :p1 - p0, :] = wdr[p0:p1].transpose(2, 1, 0, 3)
    return {"gn": np.ascontiguousarray(gn), "wup": np.ascontiguousarray(wupb), "cw": np.ascontiguousarray(cwv.reshape(128, NP * 8)),
            "wdn": np.ascontiguousarray(wd.reshape(32, 128, 11 * 256))}


def halo_slices(hfull_T, halo):
    Dd, S_ = hfull_T.shape
    pad = np.concatenate([np.zeros((Dd, halo), hfull_T.dtype), hfull_T], axis=1)
    return [np.ascontiguousarray(pad[:, c * T: c * T + T + halo]) for c in range(S_ // T)]


def build_LO(KE=32):
    nc = bass.Bass("TRN2", target_bir_lowering=False)
    xT = nc.dram_tensor("xT", [D, T], F32, kind="ExternalInput").ap()
    yT = nc.dram_tensor("yT", [KE * 128, T], F32, kind="ExternalInput").ap()
    wo = nc.dram_tensor("wo", [KT, 128, KE * 128], F32, kind="ExternalInput").ap()
    hout = nc.dram_tensor("hout", [D, T], F32, kind="ExternalOutput").ap()
    xTv = xT.rearrange("(kt p) s -> p kt s", p=128)
    yTv = yT.rearrange("(kt p) s -> p kt s", p=128)
    houtv = hout.rearrange("(kt p) s -> p kt s", p=128)
    with contextlib.ExitStack() as st:
        sb = lambda name, shape, dt: st.enter_context(nc.sbuf_tensor(name, shape, dt))
        hT = sb("hT", [128, KT, T], F32)
        ysb = sb("ysb", [128, KE, T], BF16)
        wos = [sb("wo%d" % i, [128, KE * 128], BF16) for i in range(2)]
        pst = [st.enter_context(nc.psum_tensor("ps%d" % i, [128, 512], F32)) for i in range(8)]
        P = Prog(nc)
        psr = Ring([(pst[i], Buf("ps%d" % i)) for i in range(8)])
        wor = Ring([(wos[i], Buf("wo%d" % i)) for i in range(2)])
        b_h = [Buf("h%d" % k) for k in range(KT)]
        b_y = [Buf("y%d" % k) for k in range(KE)]
        for kt in range(KT):
            P.add(SP, lambda e, kt=kt: e.dma_start(out=hT[:, kt, :], in_=xTv[:, kt, :]), writes=[b_h[kt]], dma=True, sem_key=("h", kt))

        def load_w(j):
            wt, bw = wor.next()
            P.add(POOL, lambda e, wt=wt, j=j: e.dma_start(out=wt[:, :], in_=wo[j, :, :]), writes=[bw], dma=True)
            return wt, bw
        nxt = load_w(0)
        for et in range(KE):
            P.add(POOL, lambda e, et=et: e.dma_start(out=ysb[:, et, :], in_=yTv[:, et, :]), writes=[b_y[et]], dma=True, sem_key=("y", et))
        for dmt in range(KT):
            wt, bw = nxt
            if dmt + 1 < KT:
                nxt = load_w(dmt + 1)
            banks = [psr.next() for _ in range(2)]
            for et in range(KE):
                for tb, (ps, bps) in enumerate(banks):
                    P.add(PE, lambda e, et=et, ps=ps, wt=wt, tb=tb: e.matmul(
                        ps[:, :], wt[:, et * 128:(et + 1) * 128], ysb[:, et, tb * 512:(tb + 1) * 512],
                        start=(et == 0), stop=(et == KE - 1)), reads=[bw, b_y[et]], writes=[bps])
            for tb, (ps, bps) in enumerate(banks):
                P.add(DVE, lambda e, dmt=dmt, ps=ps, tb=tb: e.tensor_tensor(
                    hT[:, dmt, tb * 512:(tb + 1) * 512], hT[:, dmt, tb * 512:(tb + 1) * 512], ps[:, :], ALU.add),
                    excl=[bps], writes=[b_h[dmt]])
            P.add(SP, lambda e, dmt=dmt: e.dma_start(out=houtv[:, dmt, :], in_=hT[:, dmt, :]), reads=[b_h[dmt]], dma=True,
                  sem_key=("o", dmt), final=True)
        P.emit()
    return nc


def lo_weights(w_out):
    KE = w_out.shape[0] // 128
    return np.ascontiguousarray(w_out.reshape(KE, 128, KT, 128).transpose(2, 1, 0, 3).reshape(KT, 128, KE * 128))


import contextlib
import numpy as np

HALO = 128
NTA = T + HALO
NH = 32
C_BQ, C_BK, C_BV, C_SINK, C_NSL, C_DIST, C_FLAG, NCST = 0, 16, 20, 276, 308, 340, 596, 597


def build_L3():
    nc = bass.Bass("TRN2", target_bir_lowering=False)
    hin = nc.dram_tensor("hin", [D, NTA], F32, kind="ExternalInput").ap()
    gn = nc.dram_tensor("gn", [128, KT], F32, kind="ExternalInput").ap()
    wblk = nc.dram_tensor("wblk", [38, 128, 2048], F32, kind="ExternalInput").ap()
    cst = nc.dram_tensor("cst", [128, NCST], F32, kind="ExternalInput").ap()
    idn = nc.dram_tensor("idn", [128, 128], F32, kind="ExternalInput").ap()
    hout = nc.dram_tensor("hout", [D, T], F32, kind="ExternalOutput").ap()
    hinv = hin.rearrange("(kt p) s -> p kt s", p=128)
    houtv = hout.rearrange("(kt p) s -> p kt s", p=128)
    with contextlib.ExitStack() as st:
        sb = lambda name, shape, dt: st.enter_context(nc.sbuf_tensor(name, shape, dt))
        hT = sb("hT", [128, KT, NTA], F32)
        hnT = sb("hnT", [128, KT, NTA], BF16)
        oT = sb("oT", [128, KT, T], BF16)
        gsb = sb("gsb", [128, KT], F32)
        csb = sb("csb", [128, NCST], F32)
        esink = sb("esink", [128, NH], F32)
        ident = sb("ident", [128, 128], BF16)
        ones = sb("ones", [128, 128], BF16)
        sqs = [sb("sq%d" % i, [128, 512], BF16) for i in range(4)]
        rstd = sb("rstd", [128, NTA], F32)
        wbs = [sb("wb%d" % i, [128, 2048], BF16) for i in range(3)]
        kTs = [sb("kT%d" % i, [128, NTA], BF16) for i in range(4)]
        vsb = sb("vsb", [128, 9, 4, 65], BF16)
        qTs = [sb("qT%d" % i, [128, T], BF16) for i in range(2)]
        sps = [sb("sp%d" % i, [128, 256], F32) for i in range(2)]
        pTs = [sb("pT%d" % i, [128, 256], BF16) for i in range(2)]
        dens = [sb("den%d" % i, [128, 1], F32) for i in range(2)]
        otoks = [sb("otok%d" % i, [128, 128], BF16) for i in range(2)]
        pst = [st.enter_context(nc.psum_tensor("ps%d" % i, [128, 512], F32)) for i in range(8)]
        P = Prog(nc)
        psr = Ring([(pst[i], Buf("ps%d" % i)) for i in range(8)])
        sqr = Ring([(sqs[i], Buf("sq%d" % i)) for i in range(4)])
        wbr = Ring([(wbs[i], Buf("wb%d" % i)) for i in range(3)])
        b_h = [Buf("h%d" % k) for k in range(KT)]
        b_hn = [Buf("hn%d" % k) for k in range(KT)]
        b_oT = [Buf("oT%d" % k) for k in range(KT)]
        b_g, b_c, b_ones, b_rstd, b_es, b_id = Buf("g"), Buf("c"), Buf("ones"), Buf("rstd"), Buf("es"), Buf("id")
        b_kT = [Buf("kT%d" % i) for i in range(4)]
        b_v = [Buf("v%d" % i) for i in range(9)]
        b_vones = Buf("vones")
        qr = Ring([(qTs[i], Buf("qT%d" % i)) for i in range(2)])
        spr = Ring([(sps[i], pTs[i], dens[i], Buf("sp%d" % i), Buf("pT%d" % i), Buf("den%d" % i)) for i in range(2)])
        otr = Ring([(otoks[i], Buf("otok%d" % i)) for i in range(2)])

        P.add(SP, lambda e: e.dma_start(out=gsb[:, :], in_=gn[:, :]), writes=[b_g], dma=True, sem_key="g")
        P.add(SP, lambda e: e.dma_start(out=csb[:, :], in_=cst[:, :]), writes=[b_c], dma=True, sem_key="c")
        P.add(POOL, lambda e: e.dma_start(out=ident[:, :], in_=idn[:, :]), writes=[b_id], dma=True, sem_key="id")
        P.add(DVE, lambda e: e.memset(ones[:, :], 1.0), writes=[b_ones])
        for kb in range(9):
            P.add(DVE, lambda e, kb=kb: e.memset(vsb[:, kb, :, 64:65], 1.0), writes=[b_vones])
        P.add(ACT, lambda e: e.activation(esink[:, :], csb[:, C_SINK:C_SINK + NH], AF.Exp), reads=[b_c], writes=[b_es])
        for kt in range(KT):
            P.add(SP, lambda e, kt=kt: e.dma_start(out=hT[:, kt, :], in_=hinv[:, kt, :]), writes=[b_h[kt]], dma=True, sem_key=("h", kt))
        chunks = [(0, 512), (512, 1024), (1024, NTA)]
        rmsnorm_T(P, nc, psr, sqr, hT, b_h, gsb, b_g, ones, b_ones, rstd, b_rstd, hnT, b_hn, chunks)

        order = list(range(38))
        state = {"i": 0}

        def load_next():
            i = state["i"]
            if i >= 38:
                return None
            state["i"] += 1
            wt, bw = wbr.next()
            P.add(POOL, lambda e, wt=wt, i=i: e.dma_start(out=wt[:, :], in_=wblk[i, :, :]), writes=[bw], dma=True)
            return wt, bw
        pending = [load_next(), load_next()]

        def take():
            cur = pending.pop(0)
            pending.append(load_next())
            return cur

        for hk in range(4):
            wt, bw = take()
            for (c0, c1) in chunks:
                ps, bps = psr.next()
                n = c1 - c0
                for kt in range(KT):
                    P.add(PE, lambda e, kt=kt, ps=ps, wt=wt, c0=c0, c1=c1, n=n: e.matmul(
                        ps[:, 0:n], wt[:, kt * 128:(kt + 1) * 128], hnT[:, kt, c0:c1], start=(kt == 0), stop=(kt == KT - 1)),
                        reads=[bw, b_hn[kt]], writes=[bps])
                P.add(ACT, lambda e, hk=hk, ps=ps, c0=c0, c1=c1, n=n: e.activation(
                    kTs[hk][:, c0:c1], ps[:, 0:n], AF.Identity, bias=csb[:, C_BK + hk:C_BK + hk + 1]),
                    excl=[bps], reads=[b_c], writes=[b_kT[hk]])
        for half in range(2):
            wt, bw = take()
            for kb in range(9):
                ps, bps = psr.next()
                for kt in range(KT):
                    P.add(PE, lambda e, kt=kt, ps=ps, wt=wt, kb=kb: e.matmul(
                        ps[:, 0:128], hnT[:, kt, kb * 128:(kb + 1) * 128], wt[:, kt * 128:(kt + 1) * 128],
                        start=(kt == 0), stop=(kt == KT - 1)), reads=[bw, b_hn[kt]], writes=[bps])
                for hh in range(2):
                    hk = half * 2 + hh
                    P.add(DVE, lambda e, kb=kb, hk=hk, hh=hh, ps=ps: e.tensor_tensor(
                        vsb[:, kb, hk, 0:64], ps[:, hh * 64:(hh + 1) * 64], csb[:, C_BV + hk * 64:C_BV + (hk + 1) * 64], ALU.add),
                        excl=[bps], reads=[b_c], writes=[b_v[kb]])
        for j in range(16):
            wt, bw = take()
            qT, b_q = qr.next()
            hk = j // 4
            for tb in range(2):
                ps, bps = psr.next()
                c0 = HALO + tb * 512
                for kt in range(KT):
                    P.add(PE, lambda e, kt=kt, ps=ps, wt=wt, c0=c0: e.matmul(
                        ps[:, :], wt[:, kt * 128:(kt + 1) * 128], hnT[:, kt, c0:c0 + 512], start=(kt == 0), stop=(kt == KT - 1)),
                        reads=[bw, b_hn[kt]], writes=[bps])
                P.add(DVE, lambda e, qT=qT, ps=ps, tb=tb, j=j: e.tensor_scalar(
                    qT[:, tb * 512:(tb + 1) * 512], ps[:, :], csb[:, C_BQ + j:C_BQ + j + 1], 0.125, ALU.add, ALU.mult),
                    excl=[bps], reads=[b_c], writes=[b_q])
            for b in range(8):
                otok, b_ot = otr.next()
                for par in range(2):
                    h = 2 * j + par
                    rows = slice(par * 64, (par + 1) * 64)
                    sp, pT, den, b_sp, b_pT, b_den = spr.next()
                    pss, bpss = psr.next()
                    for kt2 in range(2):
                        P.add(PE, lambda e, pss=pss, kt2=kt2, hk=hk, rows=rows, b=b, qT=qT: e.matmul(
                            pss[:, kt2 * 128:(kt2 + 1) * 128], kTs[hk][rows, (b + kt2) * 128:(b + kt2 + 1) * 128],
                            qT[rows, b * 128:(b + 1) * 128], start=True, stop=True),
                            reads=[b_kT[hk], b_q], writes=[bpss])
                    P.add(DVE, lambda e, sp=sp, pss=pss, h=h: e.scalar_tensor_tensor(
                        out=sp[:, :], in0=csb[:, C_DIST:C_DIST + 256], scalar=csb[:, C_NSL + h:C_NSL + h + 1], in1=pss[:, 0:256],
                        op0=ALU.mult, op1=ALU.add), excl=[bpss], reads=[b_c], writes=[b_sp])
                    P.add(ACT, lambda e, sp=sp, pT=pT: e.activation(pT[:, :], sp[:, :], AF.Exp), reads=[b_sp], writes=[b_pT])
                    if b == 0:
                        P.add(DVE, lambda e, pT=pT: e.tensor_scalar(pT[:, 0:128], pT[:, 0:128], csb[:, C_FLAG:C_FLAG + 1], None, ALU.mult),
                              reads=[b_c], writes=[b_pT])
                    pso, bpso = psr.next()
                    for kt2 in range(2):
                        P.add(PE, lambda e, pso=pso, kt2=kt2, pT=pT, b=b, hk=hk: e.matmul(
                            pso[:, 0:65], pT[:, kt2 * 128:(kt2 + 1) * 128], vsb[:, b + kt2, hk, :], start=(kt2 == 0), stop=(kt2 == 1)),
                            reads=[b_pT, b_v[b + kt2], b_vones], writes=[bpso])
                    P.add(DVE, lambda e, den=den, pso=pso, h=h: e.tensor_tensor(den[:, :], pso[:, 64:65], esink[:, h:h + 1], ALU.add),
                          excl=[bpso], reads=[b_es], writes=[b_den])
                    P.add(DVE, lambda e, den=den: e.reciprocal(den[:, :], den[:, :]), writes=[b_den])
                    P.add(DVE, lambda e, otok=otok, pso=pso, den=den, par=par: e.tensor_scalar(
                        otok[:, par * 64:(par + 1) * 64], pso[:, 0:64], den[:, :], None, ALU.mult),
                        excl=[bpso], reads=[b_den], writes=[b_ot])
                pstt, bpst = psr.next()
                ptv = pstt.bitcast(BF16)
                P.add(PE, lambda e, ptv=ptv, otok=otok: e.transpose(ptv[:, 0:128], otok[:, :], ident[:, :]),
                      reads=[b_ot, b_id], writes=[bpst])
                P.add(ACT, lambda e, ptv=ptv, j=j, b=b: e.activation(oT[:, j, b * 128:(b + 1) * 128], ptv[:, 0:128], AF.Copy),
                      excl=[bpst], writes=[b_oT[j]])
        for dmt in range(KT):
            wt, bw = take()
            banks = [psr.next() for _ in range(2)]
            for kt in range(KT):
                for tb, (ps, bps) in enumerate(banks):
                    P.add(PE, lambda e, kt=kt, ps=ps, wt=wt, tb=tb: e.matmul(
                        ps[:, :], wt[:, kt * 128:(kt + 1) * 128], oT[:, kt, tb * 512:(tb + 1) * 512],
                        start=(kt == 0), stop=(kt == KT - 1)), reads=[bw, b_oT[kt]], writes=[bps])
            for tb, (ps, bps) in enumerate(banks):
                c0 = HALO + tb * 512
                P.add(DVE, lambda e, dmt=dmt, ps=ps, c0=c0: e.tensor_tensor(
                    hT[:, dmt, c0:c0 + 512], hT[:, dmt, c0:c0 + 512], ps[:, :], ALU.add),
                    excl=[bps], writes=[b_h[dmt]])
            P.add(SP, lambda e, dmt=dmt: e.dma_start(out=houtv[:, dmt, :], in_=hT[:, dmt, HALO:NTA]), reads=[b_h[dmt]], dma=True,
                  sem_key=("o", dmt), final=True)
        P.emit()
    return nc


def l3_weights(norm_g, w_qkv, b_qkv, sinks, w_out):
    gn = np.ascontiguousarray(norm_g.reshape(KT, 128).T)
    blocks = []

    def blk(cols):
        return cols.reshape(KT, 128, 128).transpose(1, 0, 2).reshape(128, KT * 128)
    for hk in range(4):
        kc = w_qkv[:, 2048 + hk * 64: 2048 + (hk + 1) * 64]
        blocks.append(blk(np.concatenate([kc, kc], axis=1)))
    for half in range(2):
        blocks.append(blk(w_qkv[:, 2304 + half * 128: 2304 + (half + 1) * 128]))
    for j in range(16):
        blocks.append(blk(w_qkv[:, j * 128:(j + 1) * 128]))
    for dmt in range(16):
        blocks.append(blk(w_out[:, dmt * 128:(dmt + 1) * 128]))
    wblk = np.ascontiguousarray(np.stack(blocks, 0)).astype(np.float32)
    cst = np.zeros((128, NCST), np.float32)
    cst[:, C_BQ:C_BQ + 16] = b_qkv[:2048].reshape(16, 128).T
    for hk in range(4):
        bk = b_qkv[2048 + hk * 64: 2048 + (hk + 1) * 64]
        cst[:, C_BK + hk] = np.concatenate([bk, bk])
    cst[:, C_BV:C_BV + 256] = b_qkv[2304:2560][None, :]
    cst[:, C_SINK:C_SINK + 32] = sinks[None, :]
    slopes = 2.0 ** (-8.0 * np.arange(1, 33, dtype=np.float64) / 32)
    cst[:, C_NSL:C_NSL + 32] = (-slopes)[None, :]
    k = np.arange(128)[:, None]
    q = np.arange(128)[None, :]
    dist = np.zeros((128, 256))
    for kt2 in range(2):
        d = (128 + q) - (kt2 * 128 + k)
        dist[:, kt2 * 128:(kt2 + 1) * 128] = np.where((d >= 0) & (d < 128), d, 1.0e6)
    cst[:, C_DIST:C_DIST + 256] = dist
    return {"gn": gn, "wblk": wblk, "cst": cst, "idn": np.eye(128, dtype=np.float32)}


def l3_core_inputs(wts, hin_slices):
    maps = []
    for c, s in enumerate(hin_slices):
        cst = wts["cst"].copy()
        cst[:, C_FLAG] = 0.0 if c == 0 else 1.0
        maps.append(dict(wts, cst=cst, hin=s))
    return maps


from concourse.bass_utils import run_bass_kernel_spmd

_CACHE = {}


def _prog(name, builder):
    if name not in _CACHE:
        _CACHE[name] = builder()
    return _CACHE[name]


def kernel(x, norm_mix_g, ret_w_in, ret_w_out, att_w_qkv, att_b_qkv, att_sinks, att_w_out,
           norm_ffn_g, ffn_w_up, ffn_conv_w, ffn_conv_b, ffn_w_down, final_norm_g):
    f = lambda a: np.asarray(a, dtype=np.float32)
    x = f(x)[0]
    cores = list(range(8))
    nc1 = _prog("L1", build_L1)
    maps = [l1_inputs(x, f(norm_mix_g)[0], f(ret_w_in)[0], h) for h in range(8)]
    xT = maps[0]["xT"]
    res = run_bass_kernel_spmd(nc1, maps, core_ids=cores)
    yT = np.concatenate([res.results[h]["y"].T for h in range(8)], axis=0)
    nco = _prog("LO", lambda: build_LO(32))
    wo = lo_weights(f(ret_w_out)[0])
    maps = [{"xT": np.ascontiguousarray(xT[:, c * T:(c + 1) * T]), "yT": np.ascontiguousarray(yT[:, c * T:(c + 1) * T]), "wo": wo}
            for c in cores]
    res = run_bass_kernel_spmd(nco, maps, core_ids=cores)
    h1T = np.concatenate([res.results[c]["hout"] for c in cores], axis=1)
    ncf = _prog("LF0", lambda: build_LF(False))
    wts = lf_weights(f(norm_ffn_g)[0], f(final_norm_g), f(ffn_w_up)[0], f(ffn_conv_w)[0], f(ffn_conv_b)[0], f(ffn_w_down)[0])
    sl = halo_slices(h1T, 2)
    res = run_bass_kernel_spmd(ncf, [dict(wts, hin=sl[c]) for c in cores], core_ids=cores)
    h2T = np.concatenate([res.results[c]["hout"] for c in cores], axis=1)
    nca = _prog("L3", build_L3)
    wts = l3_weights(f(norm_mix_g)[1], f(att_w_qkv)[0], f(att_b_qkv)[0], f(att_sinks)[0], f(att_w_out)[0])
    res = run_bass_kernel_spmd(nca, l3_core_inputs(wts, halo_slices(h2T, HALO)), core_ids=cores)
    h3T = np.concatenate([res.results[c]["hout"] for c in cores], axis=1)
    ncf1 = _prog("LF1", lambda: build_LF(True))
    wts = lf_weights(f(norm_ffn_g)[1], f(final_norm_g), f(ffn_w_up)[1], f(ffn_conv_w)[1], f(ffn_conv_b)[1], f(ffn_w_down)[1])
    sl = halo_slices(h3T, 2)
    res = run_bass_kernel_spmd(ncf1, [dict(wts, hin=sl[c]) for c in cores], core_ids=cores)
    outT = np.concatenate([res.results[c]["hout"] for c in cores], axis=1)
    return np.ascontiguousarray(outT.T)[None, :, :].astype(np.float32)
```
